# Optimizing a Trainium2 kernel written in Bass

```python
import jax, jax.numpy as jnp
from jax import lax
import numpy as np

D_MODEL = 1024
BATCH = 8
SEQ = 4096
DEPTH = 4

N_MIXERS = 3
GRID_W = 64
D_FF = 2816
NORM_EPS = 1e-6
HG_HEADS = 8
HG_DK = D_MODEL // HG_HEADS
HG_DV = D_MODEL // HG_HEADS
HG_CHUNK = 16
NA_HEADS = 16
NA_HEAD_DIM = D_MODEL // NA_HEADS
NA_WIN_R = 8
NA_WIN_C = 16
MLA_HEADS = 16
MLA_Q_LORA = 768
MLA_KV_LORA = 256
MLA_NOPE = 64
MLA_ROPE = 32
MLA_V = 64
ROPE_THETA = 10000.0
Q_BLOCK = 128
N_A = (DEPTH + 2) // 3
N_B = (DEPTH + 1) // 3
N_C = DEPTH // 3
NEG_INF = -1e30

kernel_name = 'hybrid_hgrn2_natten_mla_macaron_encoder'


def _rms_norm(x, g):
    xf = x.astype(jnp.float32)
    y = xf * lax.rsqrt(jnp.mean(xf * xf, axis=-1, keepdims=True) + NORM_EPS)
    return (y * g.astype(jnp.float32)).astype(x.dtype)


def _swiglu(x, w_gu, w_down):
    g, u = jnp.split(x @ w_gu, 2, axis=-1)
    return (jax.nn.silu(g) * u) @ w_down


def _rope(x, pos):
    half = x.shape[-1] // 2
    inv_freq = ROPE_THETA ** (-jnp.arange(half, dtype=jnp.float32) / half)
    ang = pos.astype(jnp.float32)[:, None] * inv_freq[None, :]
    cos = jnp.cos(ang)[:, None, :]
    sin = jnp.sin(ang)[:, None, :]
    xf = x.astype(jnp.float32)
    x1, x2 = xf[..., :half], xf[..., half:]
    return jnp.concatenate([x1 * cos - x2 * sin, x2 * cos + x1 * sin], axis=-1).astype(x.dtype)


def _gla_chunk_scan(q, k, v, logf):
    H, T, dk = q.shape
    dv = v.shape[-1]
    n = T // HG_CHUNK
    q = q.reshape(H, n, HG_CHUNK, dk)
    k = k.reshape(H, n, HG_CHUNK, dk)
    v = v.reshape(H, n, HG_CHUNK, dv)
    b = jnp.cumsum(logf.reshape(H, n, HG_CHUNK, dk), axis=2)
    tri = np.tril(np.ones((HG_CHUNK, HG_CHUNK), dtype=bool))[:, :, None]
    diff = b[:, :, :, None, :] - b[:, :, None, :, :]
    decay = jnp.where(tri, jnp.exp(jnp.where(tri, diff, 0.0)), 0.0)
    attn = jnp.einsum('hntd,hntsd,hnsd->hnts', q, decay, k)
    o_intra = jnp.einsum('hnts,hnsv->hntv', attn, v)
    b_last = b[:, :, -1, :]
    q_inter = q * jnp.exp(b)
    k_inter = k * jnp.exp(b_last[:, :, None, :] - b)
    chunk_kv = jnp.einsum('hncd,hncv->hndv', k_inter, v)

    def step(state, inp):
        q_c, dl, kv_c = inp
        o_c = jnp.einsum('hcd,hdv->hcv', q_c, state)
        state = jnp.exp(dl)[..., None] * state + kv_c
        return state, o_c

    s0 = jnp.zeros((H, dk, dv), q.dtype)
    _, o_inter = lax.scan(step, s0, (q_inter.transpose(1, 0, 2, 3), b_last.transpose(1, 0, 2), chunk_kv.transpose(1, 0, 2, 3)))
    o = o_intra + o_inter.transpose(1, 0, 2, 3)
    return o.reshape(H, T, dv)


def _hgrn2_mixer(h, w_in, g_norm, w_out, lb):
    B, T, _ = h.shape
    q, zf, zb, i, g = jnp.split(h @ w_in, 5, axis=-1)

    def to_heads(t, d):
        return t.reshape(B, T, HG_HEADS, d).transpose(0, 2, 1, 3).astype(jnp.float32)

    lbh = lb.astype(jnp.float32).reshape(HG_HEADS, 1, HG_DK)
    log_lb = jnp.log(lbh)
    log_1m_lb = jnp.log1p(-lbh)

    def forget(z):
        logf = jnp.logaddexp(log_lb, log_1m_lb + jax.nn.log_sigmoid(to_heads(z, HG_DK)))
        return 1.0 - jnp.exp(logf), logf

    k_f, logf_f = forget(zf)
    k_b, logf_b = forget(zb)
    qh = to_heads(q, HG_DK) * (HG_DK ** -0.5)
    vh = to_heads(i, HG_DV)

    def per_seq(a):
        q_s, kf_s, lf_s, kb_s, lbw_s, v_s = a
        fwd = _gla_chunk_scan(q_s, kf_s, v_s, lf_s)
        bwd = _gla_chunk_scan(q_s[:, ::-1], kb_s[:, ::-1], v_s[:, ::-1], lbw_s[:, ::-1])[:, ::-1]
        return fwd + bwd

    o = lax.map(per_seq, (qh, k_f, logf_f, k_b, logf_b, vh))
    o = o.transpose(0, 2, 1, 3)
    gh = g.reshape(B, T, HG_HEADS, HG_DV).astype(jnp.float32)
    o = _rms_norm(o, g_norm) * jax.nn.silu(gh)
    return o.reshape(B, T, D_MODEL).astype(h.dtype) @ w_out


def _na_mixer(h, w_in, q_norm, k_norm, rpb, w_out):
    B, T, _ = h.shape
    rows = T // GRID_W
    win_r = min(NA_WIN_R, rows)
    q, k, v = jnp.split(h @ w_in, 3, axis=-1)
    shp = (B, rows, GRID_W, NA_HEADS, NA_HEAD_DIM)
    q = _rms_norm(q.reshape(shp), q_norm)
    k = _rms_norm(k.reshape(shp), k_norm)
    v = v.reshape(shp)
    cols = np.arange(GRID_W)
    col_start = np.clip(cols - NA_WIN_C // 2, 0, GRID_W - NA_WIN_C)
    col_mask = (cols[None, :] >= col_start[:, None]) & (cols[None, :] < col_start[:, None] + NA_WIN_C)
    col_off = np.clip(cols[None, :] - cols[:, None] + NA_WIN_C - 1, 0, 2 * NA_WIN_C - 2)
    rpb_cols = rpb.astype(jnp.float32)[:, :, col_off]
    scale = NA_HEAD_DIM ** -0.5

    def attend_row(r):
        r0 = jnp.clip(r - win_r // 2, 0, rows - win_r)
        q_r = lax.dynamic_index_in_dim(q, r, axis=1, keepdims=False)
        k_blk = lax.dynamic_slice_in_dim(k, r0, win_r, axis=1)
        v_blk = lax.dynamic_slice_in_dim(v, r0, win_r, axis=1)
        s = jnp.einsum('bqhd,brkhd->bhqrk', q_r, k_blk).astype(jnp.float32) * scale
        row_off = r0 + jnp.arange(win_r) - r + NA_WIN_R - 1
        bias = jnp.take(rpb_cols, row_off, axis=1).transpose(0, 2, 1, 3)
        s = jnp.where(col_mask[:, None, :], s + bias, NEG_INF)
        p = jax.nn.softmax(s.reshape(B, NA_HEADS, GRID_W, win_r * GRID_W), axis=-1)
        p = p.reshape(B, NA_HEADS, GRID_W, win_r, GRID_W).astype(v.dtype)
        return jnp.einsum('bhqrk,brkhd->bqhd', p, v_blk)

    o = lax.map(attend_row, jnp.arange(rows))
    o = o.transpose(1, 0, 2, 3, 4).reshape(B, T, D_MODEL)
    return o @ w_out


def _dense_attention_blocks(q, k, v):
    B, T, H, dqk = q.shape
    n_blk = T // Q_BLOCK
    scale = dqk ** -0.5
    q_blocks = q.reshape(B, n_blk, Q_BLOCK, H, dqk).transpose(1, 0, 2, 3, 4)

    def attend(q_b):
        s = jnp.einsum('bqhd,bkhd->bhqk', q_b, k).astype(jnp.float32) * scale
        p = jax.nn.softmax(s, axis=-1).astype(v.dtype)
        return jnp.einsum('bhqk,bkhd->bqhd', p, v)

    o = lax.map(attend, q_blocks)
    return o.transpose(1, 0, 2, 3, 4).reshape(B, T, H * v.shape[-1])


def _mla_mixer(h, w_in, q_a_norm, w_uq, kv_a_norm, w_ukv, q_norm, k_norm, w_out):
    B, T, _ = h.shape
    c = h @ w_in
    c_q = c[..., :MLA_Q_LORA]
    c_kv = c[..., MLA_Q_LORA:MLA_Q_LORA + MLA_KV_LORA]
    k_rope = c[..., MLA_Q_LORA + MLA_KV_LORA:]
    q = (_rms_norm(c_q, q_a_norm) @ w_uq).reshape(B, T, MLA_HEADS, MLA_NOPE + MLA_ROPE)
    kv = (_rms_norm(c_kv, kv_a_norm) @ w_ukv).reshape(B, T, MLA_HEADS, MLA_NOPE + MLA_V)
    k_rope = jnp.broadcast_to(k_rope[:, :, None, :], (B, T, MLA_HEADS, MLA_ROPE))
    k = jnp.concatenate([kv[..., :MLA_NOPE], k_rope], axis=-1)
    v = kv[..., MLA_NOPE:]
    q = _rms_norm(q, q_norm)
    k = _rms_norm(k, k_norm)
    pos = jnp.arange(T)
    q = jnp.concatenate([q[..., :MLA_NOPE], _rope(q[..., MLA_NOPE:], pos)], axis=-1)
    k = jnp.concatenate([k[..., :MLA_NOPE], _rope(k[..., MLA_NOPE:], pos)], axis=-1)
    return _dense_attention_blocks(q, k, v) @ w_out


def setup_inputs(seed: int = 0) -> dict:
    key = jax.random.key(seed)
    ks = iter(jax.random.split(key, 32))

    def nrm(shape, scale):
        return jax.random.normal(next(ks), shape, jnp.float32) * scale

    def gain(shape):
        return 1.0 + nrm(shape, 0.02)

    D = D_MODEL
    qk_dim = MLA_NOPE + MLA_ROPE
    return {
        'x': nrm((BATCH, SEQ, D), 1.0),
        'ffn1_norm': gain((DEPTH, D)),
        'ffn1_w_gu': nrm((DEPTH, D, 2 * D_FF), D ** -0.5),
        'ffn1_w_down': nrm((DEPTH, D_FF, D), D_FF ** -0.5),
        'mix_norm': gain((DEPTH, D)),
        'ffn2_norm': gain((DEPTH, D)),
        'ffn2_w_gu': nrm((DEPTH, D, 2 * D_FF), D ** -0.5),
        'ffn2_w_down': nrm((DEPTH, D_FF, D), D_FF ** -0.5),
        'hg_lb_logits': nrm((DEPTH, HG_HEADS * HG_DK), 0.5),
        'hg_w_in': nrm((N_A, D, 5 * D), D ** -0.5),
        'hg_g_norm': gain((N_A, HG_DV)),
        'hg_w_out': nrm((N_A, D, D), D ** -0.5),
        'na_w_in': nrm((N_B, D, 3 * D), D ** -0.5),
        'na_q_norm': gain((N_B, NA_HEAD_DIM)),
        'na_k_norm': gain((N_B, NA_HEAD_DIM)),
        'na_rpb': nrm((N_B, NA_HEADS, 2 * NA_WIN_R - 1, 2 * NA_WIN_C - 1), 0.2),
        'na_w_out': nrm((N_B, D, D), D ** -0.5),
        'mla_w_in': nrm((N_C, D, MLA_Q_LORA + MLA_KV_LORA + MLA_ROPE), D ** -0.5),
        'mla_q_a_norm': gain((N_C, MLA_Q_LORA)),
        'mla_w_uq': nrm((N_C, MLA_Q_LORA, MLA_HEADS * qk_dim), MLA_Q_LORA ** -0.5),
        'mla_kv_a_norm': gain((N_C, MLA_KV_LORA)),
        'mla_w_ukv': nrm((N_C, MLA_KV_LORA, MLA_HEADS * (MLA_NOPE + MLA_V)), MLA_KV_LORA ** -0.5),
        'mla_q_norm': gain((N_C, qk_dim)),
        'mla_k_norm': gain((N_C, qk_dim)),
        'mla_w_out': nrm((N_C, MLA_HEADS * MLA_V, D), (MLA_HEADS * MLA_V) ** -0.5),
    }


def reference(x, ffn1_norm, ffn1_w_gu, ffn1_w_down, mix_norm, ffn2_norm, ffn2_w_gu, ffn2_w_down,
              hg_lb_logits, hg_w_in, hg_g_norm, hg_w_out,
              na_w_in, na_q_norm, na_k_norm, na_rpb, na_w_out,
              mla_w_in, mla_q_a_norm, mla_w_uq, mla_kv_a_norm, mla_w_ukv, mla_q_norm, mla_k_norm, mla_w_out):
    gam = jnp.cumsum(jax.nn.softmax(hg_lb_logits.astype(jnp.float32), axis=0), axis=0)
    lb_all = gam - gam[0:1]
    ia = ib = ic = 0
    for layer in range(DEPTH):
        x = x + 0.5 * _swiglu(_rms_norm(x, ffn1_norm[layer]), ffn1_w_gu[layer], ffn1_w_down[layer])
        h = _rms_norm(x, mix_norm[layer])
        kind = layer % N_MIXERS
        if kind == 0:
            y = _hgrn2_mixer(h, hg_w_in[ia], hg_g_norm[ia], hg_w_out[ia], lb_all[layer])
            ia += 1
        elif kind == 1:
            y = _na_mixer(h, na_w_in[ib], na_q_norm[ib], na_k_norm[ib], na_rpb[ib], na_w_out[ib])
            ib += 1
        else:
            y = _mla_mixer(h, mla_w_in[ic], mla_q_a_norm[ic], mla_w_uq[ic], mla_kv_a_norm[ic], mla_w_ukv[ic],
                           mla_q_norm[ic], mla_k_norm[ic], mla_w_out[ic])
            ic += 1
        x = x + y
        x = x + 0.5 * _swiglu(_rms_norm(x, ffn2_norm[layer]), ffn2_w_gu[layer], ffn2_w_down[layer])
    return x
```

```python
import numpy as np
from contextlib import ExitStack
import concourse.bass as bass
import concourse.mybir as mybir
from concourse.bass_utils import run_bass_kernel_spmd

F32 = mybir.dt.float32
BF16 = mybir.dt.bfloat16
AF = mybir.ActivationFunctionType
ALU = mybir.AluOpType
AX = mybir.AxisListType

D = 1024
NC8 = 8
DFF = 2816
DEPTH = 4
EPS = 1e-6
SEM_LIMIT = 30000


class Counter:
    def __init__(self, k, name):
        self.k = k
        self.name = name
        self.sem = None
        self.val = 0
        self.retired = []

    def bump(self, n):
        if self.sem is None or self.val + n > SEM_LIMIT:
            if self.sem is not None:
                self.retired.append((self.sem, self.val))
            self.sem, self.val = self.k.take_sem(self.name)
            if self.val + n > SEM_LIMIT:
                self.retired.append((self.sem, self.val))
                self.sem, self.val = self.k.new_sem(self.name), 0
        self.val += n
        return (self.sem, self.val)

    def all_events(self):
        evs = list(self.retired)
        if self.sem is not None:
            evs.append((self.sem, self.val))
        return evs


class Pending:
    def __init__(self, eng):
        self.eng = eng
        self.ev = None


class Buf:
    def __init__(self, k, name):
        self.k = k
        self.name = name
        self.w = None
        self.r = {}
        self.cnts = {}
        k.bufs.append(self)

    def counter(self, q):
        kind = "dsw" if q == "pool" else "dhw"
        if kind not in self.cnts:
            self.cnts[kind] = Counter(self.k, kind)
            self.k.dma_counters.append(self.cnts[kind])
        return self.cnts[kind]


class K:
    ENG = ("pe", "act", "dve", "pool", "sp")

    def __init__(self, nc, stack):
        self.nc = nc
        self.stack = stack
        self.e = {"pe": nc.tensor, "act": nc.scalar, "dve": nc.vector,
                  "pool": nc.gpsimd, "sp": nc.sync}
        self.bufs = []
        self.dma_counters = []
        self.nsem = 0
        self.tick = {e: Counter(self, "t" + e) for e in self.ENG}
        self.pending = {e: None for e in self.ENG}
        self.seen = {e: {} for e in self.ENG}
        self.ninst = {e: 0 for e in self.ENG}
        self.bank_i = 0
        self.sem_pool = {"dsw": [], "dhw": []}

    def take_sem(self, name):
        if name in self.sem_pool and self.sem_pool[name]:
            return self.sem_pool[name].pop()
        return self.new_sem(name), 0

    def new_sem(self, name):
        self.nsem += 1
        return self.stack.enter_context(self.nc.semaphore(f"{name}{self.nsem}"))

    def buf(self, name):
        return Buf(self, name)

    def _resolve(self, ev):
        if isinstance(ev, Pending):
            if ev.ev is None:
                raise RuntimeError(f"wait on unresolved pending tick of {ev.eng}")
            return [ev.ev]
        if isinstance(ev, Counter):
            return ev.all_events()
        return [ev]

    def _wait(self, eng, evs):
        seen = self.seen[eng]
        for ev in evs:
            for (sem, val) in self._resolve(ev):
                key = id(sem)
                if seen.get(key, (None, 0))[1] >= val:
                    continue
                seen[key] = (sem, val)
                self.e[eng].wait_ge(sem, val)
                self.ninst[eng] += 1

    def _deps(self, eng, reads, writes):
        evs = []
        for b in reads:
            if b.w is not None:
                evs.append(b.w)
        for b in writes:
            if b.w is not None:
                evs.append(b.w)
            evs.extend(b.r.items())
        out = []
        for (key, ev) in evs:
            if eng == "pe" and key == "pe":
                continue
            out.append(ev)
        return out

    def _record(self, key, ev, reads, writes):
        for b in reads:
            b.r[key] = ev
        for b in writes:
            b.w = (key, ev)
            b.r = {}

    def op(self, eng, fn, reads=(), writes=(), tick=True):
        self._wait(eng, self._deps(eng, reads, writes))
        inst = fn()
        self.ninst[eng] += 1
        if tick:
            ev = self.tick[eng].bump(1)
            inst.then_inc(ev[0], 1)
            if self.pending[eng] is not None:
                self.pending[eng].ev = ev
                self.pending[eng] = None
        else:
            if self.pending[eng] is None:
                self.pending[eng] = Pending(eng)
            ev = self.pending[eng]
        self._record(eng, ev, reads, writes)
        return inst

    def dma(self, q, out, in_, reads=(), writes=(), owner=None, **kw):
        self._wait(q, self._deps(q, reads, writes))
        cnt = owner.counter(q)
        sem, _ = cnt.bump(16)
        inst = self.e[q].dma_start(out=out, in_=in_, **kw)
        inst.then_inc(sem, 16)
        self.ninst[q] += 1
        self._record("dma%d" % id(cnt), cnt, reads, writes)
        return inst

    def barrier(self):
        for e in self.ENG:
            assert self.pending[e] is None, f"pending tick on {e} at barrier"
        evs = [self.tick[e] for e in self.ENG] + list(self.dma_counters)
        for e in self.ENG:
            self._wait(e, evs)
        for b in self.bufs:
            b.w = None
            b.r = {}
        keep = []
        for b in self.bufs:
            if getattr(b, "keep", False):
                keep.append(b)
            else:
                for kind, cnt in b.cnts.items():
                    if cnt.sem is not None:
                        self.sem_pool[kind].append((cnt.sem, cnt.val))
                    self.dma_counters.remove(cnt)
        self.bufs = keep


class Ctx:
    pass


def sb(c, st, name, shape, dt):
    c.uid = getattr(c, "uid", 0) + 1
    name = f"{name}_u{c.uid}"
    t = st.enter_context(c.nc.sbuf_tensor(name, list(shape), dt))
    return t, c.k.buf(name)


def next_bank(c):
    rot = c.rot
    i = rot[c.k.bank_i % len(rot)]
    c.k.bank_i += 1
    return c.ps[i], c.psb[i]


def reserve_banks(c, n):
    held = c.rot[-n:]
    c.rot = c.rot[:-n]
    return [(c.ps[i], c.psb[i]) for i in held], held


def release_banks(c, held):
    c.rot = c.rot + list(held)


class WStream:
    def __init__(self, c, st, name, nelem, nbuf=2):
        self.c = c
        self.nelem = nelem
        self.nbuf = nbuf
        self.stg = [sb(c, st, f"{name}_s{i}", [128, nelem], F32) for i in range(nbuf)]
        self.wbf = [sb(c, st, f"{name}_b{i}", [128, nelem], BF16) for i in range(nbuf)]
        self.specs = []
        self.n_load = 0
        self.n_cast = 0

    def add(self, specs):
        self.specs.extend(specs)

    def _load(self):
        i = self.n_load
        if i >= len(self.specs):
            return
        used, parts = self.specs[i]
        t, b = self.stg[i % self.nbuf]
        for (vf, src) in parts:
            self.c.k.dma("sp", vf(t), src, writes=[b], owner=b)
        self.n_load += 1

    def _cast(self):
        i = self.n_cast
        if i >= len(self.specs):
            return
        while self.n_load <= i:
            self._load()
        used, parts = self.specs[i]
        t, b = self.stg[i % self.nbuf]
        w, wb = self.wbf[i % self.nbuf]
        nc = self.c.nc
        self.c.k.op("pool", lambda: nc.gpsimd.tensor_copy(out=w[:, 0:used], in_=t[:, 0:used]),
                    reads=[b], writes=[wb])
        self.n_cast += 1

    def get(self, i):
        while self.n_cast <= i:
            self._cast()
        while self.n_load < min(len(self.specs), i + self.nbuf):
            self._load()
        return self.wbf[i % self.nbuf]

    def prefetch(self, i):
        while self.n_cast <= i and self.n_cast < len(self.specs):
            self._cast()
        while self.n_load < min(len(self.specs), i + self.nbuf):
            self._load()


def emit_transpose_in(c, x_dram, T):
    nc, k = c.nc, c.k
    with ExitStack() as st:
        xin = [sb(c, st, f"tin{i}", [128, 4, D], F32) for i in range(2)]
        xo = [sb(c, st, f"tout{i}", [128, NC8, 512], F32) for i in range(2)]
        for g in range(T // 512):
            xi, xib = xin[g % 2]
            xt, xtb = xo[g % 2]
            src = x_dram[g * 512:(g + 1) * 512, :].rearrange("(t p) f -> p t f", p=128)
            k.dma("sp", xi[:], src, writes=[xib], owner=xib)
            for cc in range(NC8):
                ps, psb = next_bank(c)
                for tt in range(4):
                    k.op("pe", lambda: nc.tensor.transpose(ps[:, tt * 128:(tt + 1) * 128],
                                                           xi[:, tt, cc * 128:(cc + 1) * 128], c.ident[:]),
                         reads=[xib, c.identb], writes=[psb], tick=(tt == 3))
                eng = "act" if cc % 2 == 0 else "dve"
                if eng == "act":
                    k.op("act", lambda: nc.scalar.copy(out=xt[:, cc, :], in_=ps[:]), reads=[psb], writes=[xtb])
                else:
                    k.op("dve", lambda: nc.vector.tensor_copy(out=xt[:, cc, :], in_=ps[:]), reads=[psb], writes=[xtb])
            dst = c.xT.rearrange("(c p) t -> p c t", p=128)[:, :, g * 512:(g + 1) * 512]
            k.dma("pool", dst, xt[:], reads=[xtb], owner=xtb)
        k.barrier()


def emit_transpose_out(c, out_dram, T):
    nc, k = c.nc, c.k
    with ExitStack() as st:
        xin = [sb(c, st, f"uin{i}", [128, NC8, 512], F32) for i in range(2)]
        xo = [sb(c, st, f"uout{i}", [128, 4, D], F32) for i in range(2)]
        for g in range(T // 512):
            xi, xib = xin[g % 2]
            xt, xtb = xo[g % 2]
            src = c.xT.rearrange("(c p) t -> p c t", p=128)[:, :, g * 512:(g + 1) * 512]
            k.dma("sp", xi[:], src, writes=[xib], owner=xib)
            n = 0
            for tt in range(4):
                for half in range(2):
                    ps, psb = next_bank(c)
                    for q in range(4):
                        cc = half * 4 + q
                        k.op("pe", lambda: nc.tensor.transpose(ps[:, q * 128:(q + 1) * 128],
                                                               xi[:, cc, tt * 128:(tt + 1) * 128], c.ident[:]),
                             reads=[xib, c.identb], writes=[psb], tick=(q == 3))
                    if n % 2 == 0:
                        k.op("act", lambda: nc.scalar.copy(out=xt[:, tt, half * 512:(half + 1) * 512], in_=ps[:]),
                             reads=[psb], writes=[xtb])
                    else:
                        k.op("dve", lambda: nc.vector.tensor_copy(out=xt[:, tt, half * 512:(half + 1) * 512], in_=ps[:]),
                             reads=[psb], writes=[xtb])
                    n += 1
            dst = out_dram[g * 512:(g + 1) * 512, :].rearrange("(t p) f -> p t f", p=128)
            k.dma("pool", dst, xt[:], reads=[xtb], owner=xtb)
        k.barrier()


def load_gain_col(c, st, name, vec_dram):
    t, b = sb(c, st, name, [128, NC8], F32)
    c.k.dma("sp", t[:], vec_dram.rearrange("(c p) -> p c", p=128), writes=[b], owner=b,
            allow_slow_non_contiguous=True)
    return t, b


class NormRes:
    def __init__(self, c, st, name):
        self.xn = [sb(c, st, f"{name}_xn{i}", [128, NC8, 512], F32) for i in range(2)]
        self.sq = sb(c, st, f"{name}_sq", [128, NC8, 512], BF16)
        self.rstd = sb(c, st, f"{name}_rstd", [128, 512], F32)
        self.i = 0


def emit_norm(c, nr, gcol, gcolb, sub, hT, hTb, hoff):
    nc, k = c.nc, c.k
    xn, xnb = nr.xn[nr.i % 2]
    nr.i += 1
    sq, sqb = nr.sq
    rstd, rstdb = nr.rstd
    src = c.xT.rearrange("(c p) t -> p c t", p=128)[:, :, sub * 512:(sub + 1) * 512]
    k.dma("sp", xn[:], src, writes=[xnb], owner=xnb)
    k.op("pool", lambda: nc.gpsimd.tensor_tensor(out=sq[:], in0=xn[:], in1=xn[:], op=ALU.mult),
         reads=[xnb], writes=[sqb])
    ps, psb = next_bank(c)
    for cc in range(NC8):
        k.op("pe", lambda: nc.tensor.matmul(ps[:], lhsT=c.ones_bf[:], rhs=sq[:, cc, :],
                                            start=(cc == 0), stop=(cc == NC8 - 1)),
             reads=[sqb, c.ones_bfb], writes=[psb], tick=(cc == NC8 - 1))
    k.op("dve", lambda: nc.vector.tensor_scalar(out=rstd[:], in0=ps[:], scalar1=EPS, scalar2=None,
                                                op0=ALU.add),
         reads=[psb], writes=[rstdb])
    k.op("act", lambda: nc.scalar.activation(out=rstd[:], in_=rstd[:], func=AF.Sqrt), reads=[rstdb], writes=[rstdb])
    k.op("dve", lambda: nc.vector.reciprocal(out=rstd[:], in_=rstd[:]), reads=[rstdb], writes=[rstdb])
    for cc in range(NC8):
        eng = "dve"
        e = nc.vector
        k.op(eng, lambda: e.scalar_tensor_tensor(out=hT[:, cc, hoff:hoff + 512], in0=xn[:, cc, :],
                                                 scalar=gcol[:, cc:cc + 1], in1=rstd[:],
                                                 op0=ALU.mult, op1=ALU.mult),
             reads=[xnb, rstdb, gcolb], writes=[hTb])


def emit_ffn(c, T, gain_dram, wgu, wdown):
    nc, k = c.nc, c.k
    G = min(2048, T)
    NS = G // 512
    NJ = DFF // 128 // 2
    with ExitStack() as st:
        gcol, gcolb = load_gain_col(c, st, "f_g", gain_dram)
        nr = NormRes(c, st, "f")
        hT, hTb = sb(c, st, "f_hT", [128, NC8, G], BF16)
        actT, actTb = sb(c, st, "f_act", [128, NJ, G], BF16)
        tmp = [sb(c, st, f"f_tmp{i}", [128, 512], F32) for i in range(2)]
        xt = [sb(c, st, f"f_xt{i}", [128, 512], F32) for i in range(4)]
        ws = WStream(c, st, "f_w", 2 * NC8 * 128)
        wgu_v = wgu.rearrange("(c p) f -> p c f", p=128)
        wd_v = wdown.rearrange("(j p) f -> p j f", p=128)
        specs = []
        for tg in range(T // G):
            for fh in range(2):
                for jj in range(NJ):
                    j = fh * NJ + jj
                    specs.append((2048, [
                        (lambda t: t[:, 0:1024].rearrange("p (c m) -> p c m", c=NC8), wgu_v[:, :, j * 128:(j + 1) * 128]),
                        (lambda t: t[:, 1024:2048].rearrange("p (c m) -> p c m", c=NC8),
                         wgu_v[:, :, DFF + j * 128:DFF + (j + 1) * 128]),
                    ]))
                for m in range(NC8):
                    specs.append((NJ * 128, [
                        (lambda t: t[:, 0:NJ * 128].rearrange("p (j m) -> p j m", j=NJ),
                         wd_v[:, fh * NJ:(fh + 1) * NJ, m * 128:(m + 1) * 128]),
                    ]))
        ws.add(specs)
        xTv = c.xT.rearrange("(c p) t -> p c t", p=128)
        wi = 0
        ti = 0
        xi = 0
        for tg in range(T // G):
            dbufs = {}
            for fh in range(2):
                if fh == 0:
                    for s in range(NS):
                        emit_norm(c, nr, gcol, gcolb, tg * NS + s, hT, hTb, s * 512)
                for jj in range(NJ):
                    w, wb = ws.get(wi)
                    wi += 1
                    wg = w[:, 0:1024].rearrange("p (c m) -> p c m", c=NC8)
                    wu = w[:, 1024:2048].rearrange("p (c m) -> p c m", c=NC8)
                    for s in range(NS):
                        pg, pgb = next_bank(c)
                        pu, pub = next_bank(c)
                        for cc in range(NC8):
                            k.op("pe", lambda: nc.tensor.matmul(pg[:], lhsT=wg[:, cc, :], rhs=hT[:, cc, s * 512:(s + 1) * 512],
                                                                start=(cc == 0), stop=(cc == NC8 - 1)),
                                 reads=[wb, hTb], writes=[pgb], tick=(cc == NC8 - 1))
                        for cc in range(NC8):
                            k.op("pe", lambda: nc.tensor.matmul(pu[:], lhsT=wu[:, cc, :], rhs=hT[:, cc, s * 512:(s + 1) * 512],
                                                                start=(cc == 0), stop=(cc == NC8 - 1)),
                                 reads=[wb, hTb], writes=[pub], tick=(cc == NC8 - 1))
                        tm, tmb = tmp[ti % 2]
                        ti += 1
                        k.op("act", lambda: nc.scalar.activation(out=tm[:], in_=pg[:], func=AF.Silu),
                             reads=[pgb], writes=[tmb])
                        k.op("dve", lambda: nc.vector.tensor_tensor(out=actT[:, jj, s * 512:(s + 1) * 512], in0=pu[:], in1=tm[:],
                                                                    op=ALU.mult),
                             reads=[pub, tmb], writes=[actTb])
                    ws.prefetch(wi)
                for m in range(NC8):
                    w, wb = ws.get(wi)
                    wi += 1
                    wd = w[:, 0:NJ * 128].rearrange("p (j m) -> p j m", j=NJ)
                    for s in range(NS):
                        t0 = (tg * NS + s) * 512
                        x_, xb_ = xt[xi % 4]
                        xi += 1
                        key = (m, s)
                        if key not in dbufs:
                            dbufs[key] = k.buf(f"xTd{m}_{s}")
                        k.dma("sp", x_[:], xTv[:, m, t0:t0 + 512], reads=[dbufs[key]], writes=[xb_], owner=xb_)
                        py, pyb = next_bank(c)
                        for jj in range(NJ):
                            k.op("pe", lambda: nc.tensor.matmul(py[:], lhsT=wd[:, jj, :], rhs=actT[:, jj, s * 512:(s + 1) * 512],
                                                                start=(jj == 0), stop=(jj == NJ - 1)),
                                 reads=[wb, actTb], writes=[pyb], tick=(jj == NJ - 1))
                        k.op("dve", lambda: nc.vector.scalar_tensor_tensor(out=x_[:], in0=py[:], scalar=0.5, in1=x_[:],
                                                                           op0=ALU.mult, op1=ALU.add),
                             reads=[pyb, xb_], writes=[xb_])
                        k.dma("pool", xTv[:, m, t0:t0 + 512], x_[:], reads=[xb_], writes=[dbufs[key]], owner=xb_)
                    ws.prefetch(wi)
        k.barrier()


WEIGHT_SPECS = [
    ("ffn1_norm", (DEPTH, D)), ("ffn1_w_gu", (DEPTH, D, 2 * DFF)), ("ffn1_w_down", (DEPTH, DFF, D)),
    ("mix_norm", (DEPTH, D)), ("ffn2_norm", (DEPTH, D)), ("ffn2_w_gu", (DEPTH, D, 2 * DFF)),
    ("ffn2_w_down", (DEPTH, DFF, D)), ("hg_lb_logits", (DEPTH, 1024)), ("hg_w_in", (2, D, 5 * D)),
    ("hg_g_norm", (2, 128)), ("hg_w_out", (2, D, D)), ("na_w_in", (1, D, 3 * D)), ("na_q_norm", (1, 64)),
    ("na_k_norm", (1, 64)), ("na_rpb", (1, 16, 15, 31)), ("na_w_out", (1, D, D)),
    ("mla_w_in", (1, D, 1056)), ("mla_q_a_norm", (1, 768)), ("mla_w_uq", (1, 768, 1536)),
    ("mla_kv_a_norm", (1, 256)), ("mla_w_ukv", (1, 256, 2048)), ("mla_q_norm", (1, 96)),
    ("mla_k_norm", (1, 96)), ("mla_w_out", (1, D, D)),
]


def build_program(T, plan, wnames=None):
    nc = bass.Bass("TRN2", target_bir_lowering=False)
    c = Ctx()
    c.nc = nc
    c.T = T
    W = {}
    need = set(wnames) if wnames is not None else set(n for n, _ in WEIGHT_SPECS)
    for name, shape in WEIGHT_SPECS:
        if name in need:
            W[name] = nc.dram_tensor(name, list(shape), F32, kind="ExternalInput").ap()
    x_in = nc.dram_tensor("x", [T, D], F32, kind="ExternalInput").ap()
    ident_d = nc.dram_tensor("ident", [128, 128], F32, kind="ExternalInput").ap()
    out_d = nc.dram_tensor("out", [T, D], F32, kind="ExternalOutput").ap()
    c.xT = nc.dram_tensor("xT_scratch", [D, T], F32, kind="Internal").ap()
    c.W = W
    c.scratch = {}
    c.na_bias_d = nc.dram_tensor("na_bias", [128, 16 * 14 * 64], F32, kind="ExternalInput").ap()
    c.hg_mask_d = nc.dram_tensor("hg_mask", [2, 128, 128], F32, kind="ExternalInput").ap()
    c.hg_cmask_d = nc.dram_tensor("hg_cmask", [128, 4], F32, kind="ExternalInput").ap()
    c.rope_C_d = nc.dram_tensor("rope_C", [96, T], F32, kind="ExternalInput").ap()
    c.rope_S_d = nc.dram_tensor("rope_S", [96, T], F32, kind="ExternalInput").ap()
    c.rope_R_d = nc.dram_tensor("rope_R", [96, 96], F32, kind="ExternalInput").ap()
    with ExitStack() as st:
        k = K(nc, st)
        c.k = k
        c.ps = []
        c.psb = []
        c.rot = list(range(8))
        for i in range(8):
            p = st.enter_context(nc.psum_tensor(f"bank{i}", [128, 512], F32))
            c.ps.append(p)
            b = k.buf(f"bank{i}")
            b.keep = True
            c.psb.append(b)
        c.ident, c.identb = sb(c, st, "ident_sb", [128, 128], F32)
        c.identb.keep = True
        c.ones_bf, c.ones_bfb = sb(c, st, "ones_bf", [128, 128], BF16)
        c.ones_bfb.keep = True
        k.dma("sp", c.ident[:], ident_d[:, :], writes=[c.identb], owner=c.identb)
        k.op("pool", lambda: nc.gpsimd.memset(c.ones_bf[:], 1.0 / 1024.0), writes=[c.ones_bfb])
        k.barrier()
        for ph in plan:
            if ph[0] == "tin":
                emit_transpose_in(c, x_in, T)
            elif ph[0] == "tout":
                emit_transpose_out(c, out_d, T)
            elif ph[0] == "ffn1":
                L = ph[1]
                emit_ffn(c, T, W["ffn1_norm"][L], W["ffn1_w_gu"][L], W["ffn1_w_down"][L])
            elif ph[0] == "ffn2":
                L = ph[1]
                emit_ffn(c, T, W["ffn2_norm"][L], W["ffn2_w_gu"][L], W["ffn2_w_down"][L])
            elif ph[0] == "mix":
                L = ph[1]
                if L % 3 == 0:
                    emit_hgrn(c, T, L, L // 3)
                elif L % 3 == 1:
                    emit_na(c, T, L)
                elif L % 3 == 2:
                    emit_mla(c, T, L)
                else:
                    raise ValueError(ph)
            else:
                raise ValueError(ph)
        k.barrier()
        c.ninst = dict(k.ninst)
        c.nsem = k.nsem
    return nc, c


def host_consts(T, na_rpb):
    C, S, Rl = make_rope_tables(T)
    mf, mb = make_hg_masks()
    return {
        "ident": np.eye(128, dtype=np.float32),
        "na_bias": make_na_bias(np.asarray(na_rpb, dtype=np.float32)),
        "rope_C": C, "rope_S": S, "rope_R": Rl,
        "hg_mask": np.ascontiguousarray(np.stack([mf, mb])),
        "hg_cmask": np.ascontiguousarray((np.arange(128)[:, None] // 32 == np.arange(4)[None, :]).astype(np.float32)),
    }


def full_plan():
    plan = [("tin",)]
    for L in range(DEPTH):
        plan += [("ffn1", L), ("mix", L), ("ffn2", L)]
    plan.append(("tout",))
    return plan


def kernel(**inputs):
    x = np.ascontiguousarray(np.asarray(inputs["x"], dtype=np.float32))
    B, T, _ = x.shape
    nc, c = build_program(T, full_plan())
    shared = {name: np.ascontiguousarray(np.asarray(inputs[name], dtype=np.float32)) for name, _ in WEIGHT_SPECS}
    shared.update(host_consts(T, inputs["na_rpb"][0]))
    in_maps = []
    for b in range(B):
        m = dict(shared)
        m["x"] = x[b]
        in_maps.append(m)
    res = run_bass_kernel_spmd(nc, in_maps, core_ids=list(range(B)))
    return np.stack([np.asarray(r["out"], dtype=np.float32) for r in res.results], axis=0)


def dram_scratch(c, name, shape, dt):
    if name not in c.scratch:
        c.scratch[name] = c.nc.dram_tensor(name, list(shape), dt, kind="Internal").ap()
    return c.scratch[name]


def emit_outproj(c, T, oT_d, w_out):
    nc, k = c.nc, c.k
    G = min(2048, T)
    NS = G // 512
    with ExitStack() as st:
        oT, oTb = sb(c, st, "op_oT", [128, NC8, G], BF16)
        xt = [sb(c, st, f"op_xt{i}", [128, 512], F32) for i in range(4)]
        ws = WStream(c, st, "op_w", NC8 * 128)
        wv = w_out.rearrange("(c p) f -> p c f", p=128)
        specs = []
        for tg in range(T // G):
            for m in range(NC8):
                specs.append((1024, [(lambda t: t[:, 0:1024].rearrange("p (c m) -> p c m", c=NC8),
                                      wv[:, :, m * 128:(m + 1) * 128])]))
        ws.add(specs)
        xTv = c.xT.rearrange("(c p) t -> p c t", p=128)
        oTv = oT_d.rearrange("(c p) t -> p c t", p=128)
        wi = 0
        xi = 0
        for tg in range(T // G):
            k.dma("sp", oT[:], oTv[:, :, tg * G:(tg + 1) * G], writes=[oTb], owner=oTb)
            for m in range(NC8):
                w, wb = ws.get(wi)
                wi += 1
                wm = w[:, 0:1024].rearrange("p (c m) -> p c m", c=NC8)
                for s in range(NS):
                    t0 = tg * G + s * 512
                    x_, xb_ = xt[xi % 4]
                    xi += 1
                    k.dma("sp", x_[:], xTv[:, m, t0:t0 + 512], writes=[xb_], owner=xb_)
                    py, pyb = next_bank(c)
                    for cc in range(NC8):
                        k.op("pe", lambda: nc.tensor.matmul(py[:], lhsT=wm[:, cc, :], rhs=oT[:, cc, s * 512:(s + 1) * 512],
                                                            start=(cc == 0), stop=(cc == NC8 - 1)),
                             reads=[wb, oTb], writes=[pyb], tick=(cc == NC8 - 1))
                    k.op("dve", lambda: nc.vector.tensor_tensor(out=x_[:], in0=py[:], in1=x_[:], op=ALU.add),
                         reads=[pyb, xb_], writes=[xb_])
                    k.dma("pool", xTv[:, m, t0:t0 + 512], x_[:], reads=[xb_], owner=xb_)
                ws.prefetch(wi)
        k.barrier()


def emit_headnorm(c, raw, rawb, sq, sqb, rstd, rstdb, onesm, onesmb, gcol, gcolb, out_ap, outb, P, N, post=None):
    nc, k = c.nc, c.k
    k.op("pool", lambda: nc.gpsimd.tensor_tensor(out=sq[0:P, 0:N], in0=raw[0:P, 0:N], in1=raw[0:P, 0:N], op=ALU.mult),
         reads=[rawb], writes=[sqb])
    ps, psb = next_bank(c)
    k.op("pe", lambda: nc.tensor.matmul(ps[0:P, 0:N], lhsT=onesm[0:P, 0:P], rhs=sq[0:P, 0:N], start=True, stop=True),
         reads=[sqb, onesmb], writes=[psb])
    k.op("dve", lambda: nc.vector.tensor_scalar(out=rstd[0:P, 0:N], in0=ps[0:P, 0:N], scalar1=EPS, scalar2=None, op0=ALU.add),
         reads=[psb], writes=[rstdb])
    k.op("act", lambda: nc.scalar.activation(out=rstd[0:P, 0:N], in_=rstd[0:P, 0:N], func=AF.Sqrt), reads=[rstdb], writes=[rstdb])
    k.op("dve", lambda: nc.vector.reciprocal(out=rstd[0:P, 0:N], in_=rstd[0:P, 0:N]), reads=[rstdb], writes=[rstdb])
    k.op("dve", lambda: nc.vector.scalar_tensor_tensor(out=out_ap, in0=raw[0:P, 0:N], scalar=gcol[0:P, 0:1], in1=rstd[0:P, 0:N],
                                                       op0=ALU.mult, op1=ALU.mult),
         reads=[rawb, rstdb, gcolb], writes=[outb])


def emit_na(c, T, L, ib=0):
    nc, k = c.nc, c.k
    W = c.W
    rows = T // 64
    NH = 16
    qT_d = dram_scratch(c, "na_qT", [D, T], BF16)
    kT_d = dram_scratch(c, "na_kT", [D, T], BF16)
    v_d = dram_scratch(c, "na_v", [T, NH * 65], BF16)
    oT_d = dram_scratch(c, "mix_oT", [D, T], BF16)
    w_in = W["na_w_in"][ib]
    with ExitStack() as st:
        gcol, gcolb = load_gain_col(c, st, "na_g", W["mix_norm"][L])
        nr = NormRes(c, st, "na")
        hT, hTb = sb(c, st, "na_hT", [128, NC8, T], BF16)
        for s in range(T // 512):
            emit_norm(c, nr, gcol, gcolb, s, hT, hTb, s * 512)
        gq, gqb = sb(c, st, "na_gq", [128, 1], F32)
        gk, gkb = sb(c, st, "na_gk", [128, 1], F32)
        for half in range(2):
            k.dma("sp", gq[half * 64:(half + 1) * 64, :], W["na_q_norm"][ib].rearrange("(p o) -> p o", o=1),
                  writes=[gqb], owner=gqb)
            k.dma("sp", gk[half * 64:(half + 1) * 64, :], W["na_k_norm"][ib].rearrange("(p o) -> p o", o=1),
                  writes=[gkb], owner=gkb)
        bones, bonesb = sb(c, st, "na_bones", [128, 128], BF16)
        k.op("pool", lambda: nc.gpsimd.memset(bones[:], 0.0), writes=[bonesb])
        k.op("pool", lambda: nc.gpsimd.memset(bones[0:64, 0:64], 1.0 / 64.0), writes=[bonesb])
        k.op("pool", lambda: nc.gpsimd.memset(bones[64:128, 64:128], 1.0 / 64.0), writes=[bonesb])
        raw = [sb(c, st, f"na_raw{i}", [128, 512], F32) for i in range(2)]
        sq = [sb(c, st, f"na_sq{i}", [128, 512], BF16) for i in range(2)]
        rs = [sb(c, st, f"na_rs{i}", [128, 512], F32) for i in range(2)]
        ob = [sb(c, st, f"na_ob{i}", [128, 512], BF16) for i in range(3)]
        vt_ = [sb(c, st, f"na_vt{i}", [128, 4, 65], BF16) for i in range(4)]
        for (t_, b_) in vt_:
            k.op("pool", lambda: nc.gpsimd.memset(t_[:], 1.0), writes=[b_])
        ws = WStream(c, st, "na_w", NC8 * 256)
        wv = w_in.rearrange("(c p) f -> p c f", p=128)
        specs = []
        for ch in range(16):
            specs.append((1024, [(lambda t: t[:, 0:1024].rearrange("p (c m) -> p c m", c=NC8),
                                  wv[:, :, ch * 128:(ch + 1) * 128])]))
        for vt in range(4):
            specs.append((2048, [(lambda t: t[:, 0:2048].rearrange("p (c m) -> p c m", c=NC8),
                                  wv[:, :, 2048 + vt * 256:2048 + (vt + 1) * 256])]))
        ws.add(specs)
        n = 0
        for ch in range(16):
            w, wb = ws.get(ch)
            wm = w[:, 0:1024].rearrange("p (c m) -> p c m", c=NC8)
            dst = (qT_d if ch < 8 else kT_d)
            g_, gb_ = (gq, gqb) if ch < 8 else (gk, gkb)
            for s in range(T // 512):
                ps, psb = next_bank(c)
                for cc in range(NC8):
                    k.op("pe", lambda: nc.tensor.matmul(ps[:], lhsT=wm[:, cc, :], rhs=hT[:, cc, s * 512:(s + 1) * 512],
                                                        start=(cc == 0), stop=(cc == NC8 - 1)),
                         reads=[wb, hTb], writes=[psb], tick=(cc == NC8 - 1))
                r_, rb_ = raw[n % 2]
                s_, sb_ = sq[n % 2]
                d_, db_ = rs[n % 2]
                o_, ob_ = ob[n % 3]
                n += 1
                k.op("act", lambda: nc.scalar.copy(out=r_[:], in_=ps[:]), reads=[psb], writes=[rb_])
                emit_headnorm(c, r_, rb_, s_, sb_, d_, db_, bones, bonesb, g_, gb_, o_[:], ob_, 128, 512)
                k.dma("pool", dst[(ch % 8) * 128:(ch % 8 + 1) * 128, s * 512:(s + 1) * 512], o_[:], reads=[ob_], owner=ob_)
            ws.prefetch(ch + 1)
        n = 0
        for vt in range(4):
            w, wb = ws.get(16 + vt)
            wm = w[:, 0:2048].rearrange("p (c m) -> p c m", c=NC8)
            for tt in range(T // 128):
                ps, psb = next_bank(c)
                for cc in range(NC8):
                    k.op("pe", lambda: nc.tensor.matmul(ps[:, 0:256], lhsT=hT[:, cc, tt * 128:(tt + 1) * 128], rhs=wm[:, cc, :],
                                                        start=(cc == 0), stop=(cc == NC8 - 1)),
                         reads=[wb, hTb], writes=[psb], tick=(cc == NC8 - 1))
                t_, b_ = vt_[n % 4]
                n += 1
                k.op("act", lambda: nc.scalar.copy(out=t_[:, :, 0:64], in_=ps[:, 0:256].rearrange("p (h d) -> p h d", h=4)),
                     reads=[psb], writes=[b_])
                k.dma("pool", v_d[tt * 128:(tt + 1) * 128, vt * 260:(vt + 1) * 260], t_[:].rearrange("p h d -> p (h d)"),
                      reads=[b_], owner=b_)
            ws.prefetch(16 + vt + 1)
        k.barrier()
    with ExitStack() as st:
        kres, kresb = sb(c, st, "na_k", [128, NC8, T], BF16)
        k.dma("sp", kres[:], kT_d.rearrange("(c p) t -> p c t", p=128), writes=[kresb], owner=kresb)
        bias, biasb = sb(c, st, "na_bias_sb", [128, NH * 14 * 64], F32)
        k.dma("sp", bias[:], c.na_bias_d[:, :], writes=[biasb], owner=biasb)
        biasv = bias[:].rearrange("p (h q i x) -> p h q i x", h=NH, q=2, i=7)
        qg = [sb(c, st, f"na_qg{i}", [128, NC8, 512], BF16) for i in range(2)]
        vw = [sb(c, st, f"na_vw{i}", [128, 4, NH * 65], BF16) for i in range(2)]
        tmp = [sb(c, st, f"na_tmp{i}", [128, 256], F32) for i in range(3)]
        pt = [sb(c, st, f"na_pt{i}", [128, 4, 64], BF16) for i in range(3)]
        rden, rdenb = sb(c, st, "na_rden", [64, NH], F32)
        orow = [sb(c, st, f"na_orow{i}", [64, D], F32) for i in range(2)]
        og = [sb(c, st, f"na_og{i}", [128, NC8, 512], BF16) for i in range(2)]
        nv = 0
        nt = 0
        last_r0 = None
        pos_all, held = reserve_banks(c, 3)
        for r in range(rows):
            g = r // 8
            q_, qb_ = qg[g % 2]
            og_, ogb_ = og[g % 2]
            if r % 8 == 0:
                k.dma("sp", q_[:], qT_d.rearrange("(c p) t -> p c t", p=128)[:, :, g * 512:(g + 1) * 512],
                      writes=[qb_], owner=qb_)
            r0 = min(max(r - 4, 0), rows - 8)
            delta = r0 - r
            par = (delta + 7) % 2
            i0 = (delta + 7 - par) // 2
            if r0 != last_r0:
                v_, vb_ = vw[nv % 2]
                nv += 1
                k.dma("sp", v_[:], v_d[r0 * 64:r0 * 64 + 512, :].rearrange("(kc p) f -> p kc f", p=128),
                      writes=[vb_], owner=vb_)
                last_r0 = r0
            pos = [pos_all[(r % 2) * 3 + j] for j in range(3)] if False else pos_all[0:3]
            for hd in range(NH):
                ch = hd // 2
                base = (hd % 2) * 64
                ps, psb = next_bank(c)
                for kc in range(4):
                    k.op("pe", lambda: nc.tensor.matmul(ps[:, kc * 64:(kc + 1) * 64],
                                                        lhsT=kres[base:base + 64, ch, r0 * 64 + kc * 128:r0 * 64 + (kc + 1) * 128],
                                                        rhs=q_[base:base + 64, ch, (r % 8) * 64:(r % 8 + 1) * 64],
                                                        start=True, stop=True),
                         reads=[kresb, qb_], writes=[psb], tick=(kc == 3))
                tm, tmb = tmp[nt % 3]
                p_, pb_ = pt[nt % 3]
                nt += 1
                k.op("dve", lambda: nc.vector.scalar_tensor_tensor(
                    out=tm[:].rearrange("p (i x) -> p i x", i=4), in0=ps[:, 0:256].rearrange("p (i x) -> p i x", i=4),
                    scalar=0.125, in1=biasv[:, hd, par, i0:i0 + 4, :], op0=ALU.mult, op1=ALU.add),
                     reads=[psb, biasb], writes=[tmb])
                k.op("act", lambda: nc.scalar.activation(out=p_[:].rearrange("p i x -> p (i x)"), in_=tm[:], func=AF.Exp),
                     reads=[tmb], writes=[pb_])
                po, pob = pos[hd // 7]
                col = (hd % 7) * 65
                for kc in range(4):
                    k.op("pe", lambda: nc.tensor.matmul(po[0:64, col:col + 65], lhsT=p_[:, kc, :],
                                                        rhs=v_[:, kc, hd * 65:(hd + 1) * 65],
                                                        start=(kc == 0), stop=(kc == 3)),
                         reads=[pb_, vb_], writes=[pob], tick=(kc == 3))
            o_, ob_ = orow[r % 2]
            for bi in range(3):
                po, pob = pos[bi]
                nh = 7 if bi < 2 else 2
                pv = po[0:64, 0:nh * 65].rearrange("p (h d) -> p h d", h=nh)
                k.op("dve", lambda: nc.vector.reciprocal(out=rden[:, bi * 7:bi * 7 + nh], in_=pv[:, :, 64]),
                     reads=[pob], writes=[rdenb])
                k.op("dve", lambda: nc.vector.tensor_tensor(
                    out=o_[:, bi * 448:bi * 448 + nh * 64].rearrange("p (h d) -> p h d", h=nh), in0=pv[:, :, 0:64],
                    in1=rden[:, bi * 7:bi * 7 + nh].rearrange("p (h o) -> p h o", o=1).broadcast_to([64, nh, 64]), op=ALU.mult),
                     reads=[pob, rdenb], writes=[ob_])
            ps, psb = next_bank(c)
            for cc in range(NC8):
                k.op("pe", lambda: nc.tensor.transpose(ps[:, cc * 64:(cc + 1) * 64], o_[:, cc * 128:(cc + 1) * 128], c.ident[0:64, 0:64]),
                     reads=[ob_, c.identb], writes=[psb], tick=(cc == NC8 - 1))
            k.op("act", lambda: nc.scalar.copy(out=og_[:, :, (r % 8) * 64:(r % 8 + 1) * 64],
                                               in_=ps[:].rearrange("p (c x) -> p c x", c=NC8)),
                 reads=[psb], writes=[ogb_])
            if r % 8 == 7:
                k.dma("pool", oT_d.rearrange("(c p) t -> p c t", p=128)[:, :, g * 512:(g + 1) * 512], og_[:],
                      reads=[ogb_], owner=ogb_)
        release_banks(c, held)
        k.barrier()
    emit_outproj(c, T, oT_d, W["na_w_out"][ib])


def make_na_bias(rpb):
    cols = np.arange(64)
    col_start = np.clip(cols - 8, 0, 48)
    col_mask = (cols[None, :] >= col_start[:, None]) & (cols[None, :] < col_start[:, None] + 16)
    col_off = np.clip(cols[None, :] - cols[:, None] + 15, 0, 30)
    bt = rpb[:, :, col_off]
    bt = np.where(col_mask[None, None], bt, np.float32(-30000.0)).astype(np.float32)
    out = np.zeros((2, 64, 16, 2, 7, 64), np.float32)
    for par in range(2):
        for i in range(7):
            for rr in range(2):
                ro = 2 * i + par + rr
                if ro <= 14:
                    out[rr, :, :, par, i, :] = bt[:, ro].transpose(2, 0, 1)
    return np.ascontiguousarray(out.reshape(128, 16 * 2 * 7 * 64))


def make_rope_tables(T):
    half = 16
    inv_freq = (np.float32(10000.0) ** (-np.arange(half, dtype=np.float32) / np.float32(half))).astype(np.float32)
    ang = (np.arange(T, dtype=np.float32)[:, None] * inv_freq[None, :]).astype(np.float32)
    cs = np.cos(ang).astype(np.float32).T
    sn = np.sin(ang).astype(np.float32).T
    C = np.ones((96, T), np.float32)
    S = np.zeros((96, T), np.float32)
    C[64:80] = cs
    C[80:96] = cs
    S[64:80] = sn
    S[80:96] = sn
    Rl = np.zeros((96, 96), np.float32)
    for i in range(16):
        Rl[80 + i, 64 + i] = -1.0
        Rl[64 + i, 80 + i] = 1.0
    return C, S, Rl


def emit_mla(c, T, L, ic=0):
    nc, k = c.nc, c.k
    W = c.W
    NH = 16
    q_d = dram_scratch(c, "mla_q", [NH, 96, T], BF16)
    k_d = dram_scratch(c, "mla_k", [NH, 96, T], BF16)
    v_d = dram_scratch(c, "mla_v", [T, NH * 65], BF16)
    oT_d = dram_scratch(c, "mix_oT", [D, T], BF16)
    w_in = W["mla_w_in"][ic]
    w_uq = W["mla_w_uq"][ic]
    w_ukv = W["mla_w_ukv"][ic]
    with ExitStack() as st:
        gcol, gcolb = load_gain_col(c, st, "ml_g", W["mix_norm"][L])
        nr = NormRes(c, st, "ml")
        hT, hTb = sb(c, st, "ml_hT", [128, NC8, 512], BF16)
        ga, gab = sb(c, st, "ml_ga", [128, 8], F32)
        k.dma("sp", ga[:, 0:6], W["mla_q_a_norm"][ic].rearrange("(c p) -> p c", p=128), writes=[gab], owner=gab,
              allow_slow_non_contiguous=True)
        k.dma("sp", ga[:, 6:8], W["mla_kv_a_norm"][ic].rearrange("(c p) -> p c", p=128), writes=[gab], owner=gab,
              allow_slow_non_contiguous=True)
        gq, gqb = sb(c, st, "ml_gq", [96, 1], F32)
        gk, gkb = sb(c, st, "ml_gk", [96, 1], F32)
        k.dma("sp", gq[:], W["mla_q_norm"][ic].rearrange("(p o) -> p o", o=1), writes=[gqb], owner=gqb)
        k.dma("sp", gk[:], W["mla_k_norm"][ic].rearrange("(p o) -> p o", o=1), writes=[gkb], owner=gkb)
        ones1, ones1b = sb(c, st, "ml_ones1", [128, 128], BF16)
        k.op("pool", lambda: nc.gpsimd.memset(ones1[:], 1.0), writes=[ones1b])
        ones96, ones96b = sb(c, st, "ml_ones96", [96, 96], BF16)
        k.op("pool", lambda: nc.gpsimd.memset(ones96[:], 1.0 / 96.0), writes=[ones96b])
        rl32, rl32b = sb(c, st, "ml_rl32", [96, 96], F32)
        rl, rlb = sb(c, st, "ml_rl", [96, 96], BF16)
        k.dma("sp", rl32[:], c.rope_R_d[:, :], writes=[rl32b], owner=rl32b)
        k.op("pool", lambda: nc.gpsimd.tensor_copy(out=rl[:], in_=rl32[:]), reads=[rl32b], writes=[rlb])
        wkr32, wkr32b = sb(c, st, "ml_wkr32", [128, NC8, 32], F32)
        wkr, wkrb = sb(c, st, "ml_wkr", [128, NC8, 96], BF16)
        k.dma("sp", wkr32[:], w_in.rearrange("(c p) f -> p c f", p=128)[:, :, 1024:1056], writes=[wkr32b], owner=wkr32b)
        k.op("pool", lambda: nc.gpsimd.memset(wkr[:], 0.0), writes=[wkrb])
        k.op("pool", lambda: nc.gpsimd.tensor_copy(out=wkr[:, :, 64:96], in_=wkr32[:]), reads=[wkr32b], writes=[wkrb])
        craw, crawb = sb(c, st, "ml_craw", [128, 8, 512], F32)
        csq, csqb = sb(c, st, "ml_csq", [128, 8, 512], BF16)
        cn, cnb = sb(c, st, "ml_cn", [128, 8, 512], BF16)
        rsa, rsab = sb(c, st, "ml_rsa", [128, 2, 512], F32)
        Ct, Ctb = sb(c, st, "ml_C", [96, 512], F32)
        St, Stb = sb(c, st, "ml_S", [96, 512], F32)
        raw = [sb(c, st, f"ml_raw{i}", [96, 512], F32) for i in range(2)]
        sq = [sb(c, st, f"ml_sq{i}", [96, 512], BF16) for i in range(2)]
        rs = [sb(c, st, f"ml_rs{i}", [96, 512], F32) for i in range(2)]
        qn = [sb(c, st, f"ml_qn{i}", [96, 512], BF16) for i in range(2)]
        t1 = [sb(c, st, f"ml_t1{i}", [96, 512], F32) for i in range(2)]
        t2 = [sb(c, st, f"ml_t2{i}", [96, 512], F32) for i in range(2)]
        ob = [sb(c, st, f"ml_ob{i}", [96, 512], BF16) for i in range(3)]
        vt_ = [sb(c, st, f"ml_vt{i}", [128, 8, 65], BF16) for i in range(3)]
        for (t_, b_) in vt_:
            k.op("pool", lambda: nc.gpsimd.memset(t_[:], 1.0), writes=[b_])
        ws = WStream(c, st, "ml_w", 2048)
        wsn = WStream(c, st, "ml_wn", 192)
        for (t_, b_) in wsn.stg:
            k.op("pool", lambda: nc.gpsimd.memset(t_[:], 0.0), writes=[b_])
        winv = w_in.rearrange("(c p) f -> p c f", p=128)
        wuqv = w_uq.rearrange("(c p) f -> p c f", p=128)
        wukvv = w_ukv.rearrange("(c p) f -> p c f", p=128)
        specs, specsn = [], []
        for s in range(T // 512):
            for ch in range(8):
                specs.append((1024, [(lambda t: t[:, 0:1024].rearrange("p (c m) -> p c m", c=NC8),
                                      winv[:, :, ch * 128:(ch + 1) * 128])]))
            for h in range(NH):
                specs.append((576, [(lambda t: t[:, 0:576].rearrange("p (c m) -> p c m", c=6),
                                     wuqv[:, :, h * 96:(h + 1) * 96])]))
                specsn.append((192, [(lambda t: t[:, 0:192].rearrange("p (c m) -> p c m", c=2)[:, :, 0:64],
                                      wukvv[:, :, h * 128:h * 128 + 64])]))
            for half in range(2):
                specs.append((2048, [(lambda t: t[:, 0:2048].rearrange("p (c m) -> p c m", c=2),
                                      wukvv[:, :, half * 1024:(half + 1) * 1024])]))
        ws.add(specs)
        wsn.add(specsn)
        wi = 0
        wni = 0
        n = 0
        nv = 0
        for s in range(T // 512):
            emit_norm(c, nr, gcol, gcolb, s, hT, hTb, 0)
            k.dma("sp", Ct[:], c.rope_C_d[:, s * 512:(s + 1) * 512], writes=[Ctb], owner=Ctb)
            k.dma("sp", St[:], c.rope_S_d[:, s * 512:(s + 1) * 512], writes=[Stb], owner=Stb)
            for ch in range(8):
                w, wb = ws.get(wi)
                wi += 1
                wm = w[:, 0:1024].rearrange("p (c m) -> p c m", c=NC8)
                ps, psb = next_bank(c)
                for cc in range(NC8):
                    k.op("pe", lambda: nc.tensor.matmul(ps[:], lhsT=wm[:, cc, :], rhs=hT[:, cc, :],
                                                        start=(cc == 0), stop=(cc == NC8 - 1)),
                         reads=[wb, hTb], writes=[psb], tick=(cc == NC8 - 1))
                k.op("act", lambda: nc.scalar.copy(out=craw[:, ch, :], in_=ps[:]), reads=[psb], writes=[crawb])
                ws.prefetch(wi)
            k.op("pool", lambda: nc.gpsimd.tensor_tensor(out=csq[:], in0=craw[:], in1=craw[:], op=ALU.mult),
                 reads=[crawb], writes=[csqb])
            for (lo, hi, idx, dim) in ((0, 6, 0, 768.0), (6, 8, 1, 256.0)):
                ps, psb = next_bank(c)
                for ch in range(lo, hi):
                    k.op("pe", lambda: nc.tensor.matmul(ps[:], lhsT=ones1[:], rhs=csq[:, ch, :],
                                                        start=(ch == lo), stop=(ch == hi - 1)),
                         reads=[csqb, ones1b], writes=[psb], tick=(ch == hi - 1))
                k.op("dve", lambda: nc.vector.tensor_scalar(out=rsa[:, idx, :], in0=ps[:], scalar1=1.0 / dim, scalar2=EPS,
                                                            op0=ALU.mult, op1=ALU.add),
                     reads=[psb], writes=[rsab])
            k.op("act", lambda: nc.scalar.activation(out=rsa[:], in_=rsa[:], func=AF.Sqrt), reads=[rsab], writes=[rsab])
            k.op("dve", lambda: nc.vector.reciprocal(out=rsa[:], in_=rsa[:]), reads=[rsab], writes=[rsab])
            for ch in range(8):
                idx = 0 if ch < 6 else 1
                eng = "dve"
                e = nc.vector
                k.op(eng, lambda: e.scalar_tensor_tensor(out=cn[:, ch, :], in0=craw[:, ch, :], scalar=ga[:, ch:ch + 1],
                                                         in1=rsa[:, idx, :], op0=ALU.mult, op1=ALU.mult),
                     reads=[crawb, rsab, gab], writes=[cnb])
            for h in range(NH):
                w, wb = ws.get(wi)
                wi += 1
                wq = w[:, 0:576].rearrange("p (c m) -> p c m", c=6)
                wn_, wnb_ = wsn.get(wni)
                wni += 1
                wkn = wn_[:, 0:192].rearrange("p (c m) -> p c m", c=2)
                for which in range(2):
                    ps, psb = next_bank(c)
                    if which == 0:
                        for cc in range(6):
                            k.op("pe", lambda: nc.tensor.matmul(ps[0:96, :], lhsT=wq[:, cc, :], rhs=cn[:, cc, :],
                                                                start=(cc == 0), stop=(cc == 5)),
                                 reads=[wb, cnb], writes=[psb], tick=(cc == 5))
                    else:
                        for cc in range(2):
                            k.op("pe", lambda: nc.tensor.matmul(ps[0:96, :], lhsT=wkn[:, cc, :], rhs=cn[:, 6 + cc, :],
                                                                start=(cc == 0), stop=False),
                                 reads=[wnb_, cnb], writes=[psb], tick=False)
                        for cc in range(NC8):
                            k.op("pe", lambda: nc.tensor.matmul(ps[0:96, :], lhsT=wkr[:, cc, :], rhs=hT[:, cc, :],
                                                                start=False, stop=(cc == NC8 - 1)),
                                 reads=[wkrb, hTb], writes=[psb], tick=(cc == NC8 - 1))
                    r_, rb_ = raw[n % 2]
                    s_, sb_ = sq[n % 2]
                    d_, db_ = rs[n % 2]
                    q_, qb_ = qn[n % 2]
                    a_, ab_ = t1[n % 2]
                    b2_, b2b_ = t2[n % 2]
                    o_, ob_ = ob[n % 3]
                    n += 1
                    g_, gb_ = (gq, gqb) if which == 0 else (gk, gkb)
                    k.op("act", lambda: nc.scalar.copy(out=r_[:], in_=ps[0:96, :]), reads=[psb], writes=[rb_])
                    emit_headnorm(c, r_, rb_, s_, sb_, d_, db_, ones96, ones96b, g_, gb_, q_[:], qb_, 96, 512)
                    ps2, ps2b = next_bank(c)
                    k.op("pe", lambda: nc.tensor.matmul(ps2[0:96, :], lhsT=rl[:], rhs=q_[:], start=True, stop=True),
                         reads=[rlb, qb_], writes=[ps2b])
                    k.op("pool", lambda: nc.gpsimd.tensor_tensor(out=a_[:], in0=q_[:], in1=Ct[:], op=ALU.mult),
                         reads=[qb_, Ctb], writes=[ab_])
                    k.op("dve", lambda: nc.vector.tensor_tensor(out=b2_[:], in0=ps2[0:96, :], in1=St[:], op=ALU.mult),
                         reads=[ps2b, Stb], writes=[b2b_])
                    k.op("pool", lambda: nc.gpsimd.tensor_tensor(out=o_[:], in0=a_[:], in1=b2_[:], op=ALU.add),
                         reads=[ab_, b2b_], writes=[ob_])
                    dst = q_d if which == 0 else k_d
                    k.dma("pool", dst[h, :, s * 512:(s + 1) * 512], o_[:], reads=[ob_], owner=ob_)
                ws.prefetch(wi)
                wsn.prefetch(wni)
            for half in range(2):
                w, wb = ws.get(wi)
                wi += 1
                wv_ = w[:, 0:2048].rearrange("p (c h d) -> p c h d", c=2, h=8)
                for tt in range(4):
                    ps, psb = next_bank(c)
                    for cc in range(2):
                        k.op("pe", lambda: nc.tensor.matmul(ps[:].rearrange("p (h d) -> p h d", h=8),
                                                            lhsT=cn[:, 6 + cc, tt * 128:(tt + 1) * 128],
                                                            rhs=wv_[:, cc, :, 64:128], start=(cc == 0), stop=(cc == 1)),
                             reads=[wb, cnb], writes=[psb], tick=(cc == 1))
                    t_, b_ = vt_[nv % 3]
                    nv += 1
                    k.op("act", lambda: nc.scalar.copy(out=t_[:, :, 0:64], in_=ps[:].rearrange("p (h d) -> p h d", h=8)),
                         reads=[psb], writes=[b_])
                    tok = s * 512 + tt * 128
                    k.dma("pool", v_d[tok:tok + 128, half * 520:(half + 1) * 520], t_[:].rearrange("p h d -> p (h d)"),
                          reads=[b_], owner=b_)
                ws.prefetch(wi)
        k.barrier()
    scale = float(96 ** -0.5)
    with ExitStack() as st:
        vres, vresb = sb(c, st, "ml_v", [128, T // 128, NH * 65], BF16)
        k.dma("sp", vres[:], v_d.rearrange("(kc p) f -> p kc f", p=128), writes=[vresb], owner=vresb)
        kh = [sb(c, st, f"ml_kh{i}", [96, T], BF16) for i in range(2)]
        qh = [sb(c, st, f"ml_qh{i}", [96, T], BF16) for i in range(2)]
        pt = [sb(c, st, f"ml_pt{i}", [128, 512], BF16) for i in range(3)]
        rden, rdenb = sb(c, st, "ml_rden", [128, 4], F32)
        o4 = [sb(c, st, f"ml_o4{i}", [128, 4, 64], F32) for i in range(2)]
        oh = [sb(c, st, f"ml_oh{i}", [64, 512], BF16) for i in range(2)]
        accs, held = reserve_banks(c, 2)
        npt = 0
        ng = 0
        for h in range(NH):
            k_, kb_ = kh[h % 2]
            q_, qb_ = qh[h % 2]
            k.dma("sp", k_[:], k_d[h], writes=[kb_], owner=kb_)
            k.dma("sp", q_[:], q_d[h], writes=[qb_], owner=qb_)
            for qg in range(T // 512):
                po, pob = accs[ng % 2]
                nkc = T // 128
                k.op("dve", lambda: nc.vector.memset(po[:, 0:260], 0.0), writes=[pob])
                for kc in range(nkc):
                    ps, psb = next_bank(c)
                    k.op("pe", lambda: nc.tensor.matmul(ps[:], lhsT=k_[:, kc * 128:(kc + 1) * 128], rhs=q_[:, qg * 512:(qg + 1) * 512],
                                                        start=True, stop=True),
                         reads=[kb_, qb_], writes=[psb])
                    p_, pb_ = pt[npt % 3]
                    npt += 1
                    k.op("act", lambda: nc.scalar.activation(out=p_[:], in_=ps[:], func=AF.Exp, scale=scale),
                         reads=[psb], writes=[pb_])
                    for qb in range(4):
                        k.op("pe", lambda: nc.tensor.matmul(po[:, qb * 65:(qb + 1) * 65], lhsT=p_[:, qb * 128:(qb + 1) * 128],
                                                            rhs=vres[:, kc, h * 65:(h + 1) * 65],
                                                            start=False, stop=False, skip_group_check=True),
                             reads=[pb_, vresb], writes=[pob], tick=(qb == 3))
                o_, ob_ = o4[ng % 2]
                oh_, ohb_ = oh[ng % 2]
                ng += 1
                pv = po[:, 0:260].rearrange("p (b d) -> p b d", b=4)
                k.op("dve", lambda: nc.vector.reciprocal(out=rden[:], in_=pv[:, :, 64]), reads=[pob], writes=[rdenb])
                k.op("dve", lambda: nc.vector.tensor_tensor(out=o_[:], in0=pv[:, :, 0:64],
                                                            in1=rden[:].rearrange("p (b o) -> p b o", o=1).broadcast_to([128, 4, 64]),
                                                            op=ALU.mult),
                     reads=[pob, rdenb], writes=[ob_])
                ps2, ps2b = next_bank(c)
                for qb in range(4):
                    k.op("pe", lambda: nc.tensor.transpose(ps2[0:64, qb * 128:(qb + 1) * 128], o_[:, qb, :], c.ident[:]),
                         reads=[ob_, c.identb], writes=[ps2b], tick=(qb == 3))
                k.op("act", lambda: nc.scalar.copy(out=oh_[:], in_=ps2[0:64, :]), reads=[ps2b], writes=[ohb_])
                k.dma("pool", oT_d[h * 64:(h + 1) * 64, qg * 512:(qg + 1) * 512], oh_[:], reads=[ohb_], owner=ohb_)
        release_banks(c, held)
        k.barrier()
    emit_outproj(c, T, oT_d, W["mla_w_out"][ic])


def make_hg_masks():
    s = np.arange(128)[:, None]
    t = np.arange(128)[None, :]
    same = (s // 32) == (t // 32)
    mf = (same & (s <= t)).astype(np.float32)
    mb = (same & (s >= t)).astype(np.float32)
    return mf, mb


def emit_hgrn(c, T, L, ia):
    nc, k = c.nc, c.k
    W = c.W
    NH = 8
    NT = T // 128
    NCH = T // 32
    qi_d = dram_scratch(c, "hg_qi", [2, NH, 128, T], BF16)
    kt_d = dram_scratch(c, "hg_kt", [2, NH, 128, T], BF16)
    kin_d = dram_scratch(c, "hg_kin", [2, NH, 128, T], BF16)
    dec_d = dram_scratch(c, "hg_dec", [2, NH, 128, NCH], F32)
    sg_d = dram_scratch(c, "hg_sg", [D, T], BF16)
    v_d = dram_scratch(c, "hg_v", [T, D], BF16)
    oT_d = dram_scratch(c, "mix_oT", [D, T], BF16)
    w_in = W["hg_w_in"][ia]
    with ExitStack() as st:
        gcol, gcolb = load_gain_col(c, st, "hg_g", W["mix_norm"][L])
        nr = NormRes(c, st, "hg")
        hT, hTb = sb(c, st, "hg_hT", [128, NC8, T], BF16)
        for s in range(T // 512):
            emit_norm(c, nr, gcol, gcolb, s, hT, hTb, s * 512)
        lg, lgb = sb(c, st, "hg_lg", [128, NC8, DEPTH], F32)
        for l_ in range(DEPTH):
            k.dma("sp", lg[:, :, l_], W["hg_lb_logits"][l_].rearrange("(c p) -> p c", p=128), writes=[lgb], owner=lgb,
                  allow_slow_non_contiguous=True)
        ssum, ssumb = sb(c, st, "hg_ssum", [128, NC8], F32)
        lb, lbb = sb(c, st, "hg_lb", [128, NC8], F32)
        oml, omlb = sb(c, st, "hg_oml", [128, NC8], F32)
        k.op("act", lambda: nc.scalar.activation(out=lg[:], in_=lg[:], func=AF.Exp), reads=[lgb], writes=[lgb])
        k.op("dve", lambda: nc.vector.tensor_reduce(out=ssum[:], in_=lg[:], axis=AX.X, op=ALU.add), reads=[lgb], writes=[ssumb])
        if L == 0:
            k.op("dve", lambda: nc.vector.memset(lb[:], 0.0), writes=[lbb])
        else:
            k.op("dve", lambda: nc.vector.tensor_reduce(out=lb[:], in_=lg[:, :, 1:L + 1], axis=AX.X, op=ALU.add),
                 reads=[lgb], writes=[lbb])
        k.op("dve", lambda: nc.vector.reciprocal(out=ssum[:], in_=ssum[:]), reads=[ssumb], writes=[ssumb])
        k.op("dve", lambda: nc.vector.tensor_tensor(out=lb[:], in0=lb[:], in1=ssum[:], op=ALU.mult), reads=[lbb, ssumb], writes=[lbb])
        k.op("dve", lambda: nc.vector.tensor_scalar(out=oml[:], in0=lb[:], scalar1=-1.0, scalar2=1.0, op0=ALU.mult, op1=ALU.add),
             reads=[lbb], writes=[omlb])
        rmask, rmaskb = sb(c, st, "hg_rmask", [128, 512], F32)
        k.op("pool", lambda: nc.gpsimd.memset(rmask[:], 1.0), writes=[rmaskb])
        k.op("pool", lambda: nc.gpsimd.memset(rmask[:].rearrange("p (n j) -> p n j", j=32)[:, :, 0:1], 0.0), writes=[rmaskb])

        def t512(name, dt, n):
            return [sb(c, st, f"hg_{name}{i}", [128, 512], dt) for i in range(n)]
        qf = t512("qf", F32, 2)
        ta = t512("ta", F32, 2)
        tb = t512("tb", F32, 2)
        tc_ = t512("tc", F32, 2)
        td = t512("td", F32, 2)
        te = t512("te", F32, 2)
        tf = t512("tf", F32, 2)
        o_qi = t512("oqi", BF16, 2)
        o_kt = t512("okt", BF16, 2)
        o_kin = t512("okin", BF16, 2)
        o_sg = t512("osg", BF16, 2)
        o_dec = [sb(c, st, f"hg_odec{i}", [128, 16], F32) for i in range(2)]
        o_v = [sb(c, st, f"hg_ov{i}", [128, 256], BF16) for i in range(3)]
        c.hg_qall, c.hg_qallb = sb(c, st, "hg_qall", [128, T], F32)
        ws = WStream(c, st, "hg_w", NC8 * 256)
        wv = w_in.rearrange("(c p) f -> p c f", p=128)
        specs = []
        for h in range(NH):
            for col0 in (0, 4096, 1024, 2048):
                specs.append((1024, [(lambda t: t[:, 0:1024].rearrange("p (c m) -> p c m", c=NC8),
                                      wv[:, :, col0 + h * 128:col0 + (h + 1) * 128])]))
        for vt in range(4):
            specs.append((2048, [(lambda t: t[:, 0:2048].rearrange("p (c m) -> p c m", c=NC8),
                                  wv[:, :, 3072 + vt * 256:3072 + (vt + 1) * 256])]))
        ws.add(specs)
        wi = 0
        n = 0

        def proj(wm, wb, s):
            ps, psb = next_bank(c)
            for cc in range(NC8):
                k.op("pe", lambda: nc.tensor.matmul(ps[:], lhsT=wm[:, cc, :], rhs=hT[:, cc, s * 512:(s + 1) * 512],
                                                    start=(cc == 0), stop=(cc == NC8 - 1)),
                     reads=[wb, hTb], writes=[psb], tick=(cc == NC8 - 1))
            return ps, psb

        for h in range(NH):
            wq, wqb = ws.get(wi)
            wi_q = wi
            wm = wq[:, 0:1024].rearrange("p (c m) -> p c m", c=NC8)
            for s in range(T // 512):
                ps, psb = proj(wm, wqb, s)
                k.op("act", lambda: nc.scalar.mul(out=c.hg_qall[:, s * 512:(s + 1) * 512], in_=ps[:], mul=float(128 ** -0.5)),
                     reads=[psb], writes=[c.hg_qallb])
            ws.prefetch(wi + 1)
            wi += 1
            wg, wgb = ws.get(wi)
            wm = wg[:, 0:1024].rearrange("p (c m) -> p c m", c=NC8)
            for s in range(T // 512):
                ps, psb = proj(wm, wgb, s)
                a_, ab_ = ta[n % 2]
                o_, ob_ = o_sg[n % 2]
                n += 1
                k.op("act", lambda: nc.scalar.activation(out=a_[:], in_=ps[:], func=AF.Exp, scale=-1.0), reads=[psb], writes=[ab_])
                k.op("pool", lambda: nc.gpsimd.tensor_scalar(out=a_[:], in0=a_[:], scalar1=1.0, scalar2=None, op0=ALU.add),
                     reads=[ab_], writes=[ab_])
                k.op("dve", lambda: nc.vector.reciprocal(out=a_[:], in_=a_[:]), reads=[ab_], writes=[ab_])
                k.op("dve", lambda: nc.vector.tensor_tensor(out=o_[:], in0=ps[:], in1=a_[:], op=ALU.mult),
                     reads=[psb, ab_], writes=[ob_])
                k.dma("pool", sg_d[h * 128:(h + 1) * 128, s * 512:(s + 1) * 512], o_[:], reads=[ob_], owner=ob_)
            ws.prefetch(wi + 1)
            wi += 1
            for di in range(2):
                wz, wzb = ws.get(wi)
                wm = wz[:, 0:1024].rearrange("p (c m) -> p c m", c=NC8)
                for s in range(T // 512):
                    ps, psb = proj(wm, wzb, s)
                    a_, ab_ = ta[n % 2]
                    b_, bb_ = tb[n % 2]
                    k_, kb_ = tc_[n % 2]
                    c_, cb_ = td[n % 2]
                    e_, eb_ = te[n % 2]
                    f_, fb_ = tf[n % 2]
                    oq, oqb = o_qi[n % 2]
                    ok, okb = o_kt[n % 2]
                    oi, oib = o_kin[n % 2]
                    od, odb = o_dec[n % 2]
                    n += 1
                    k.op("act", lambda: nc.scalar.activation(out=a_[:], in_=ps[:], func=AF.Exp, scale=-1.0), reads=[psb], writes=[ab_])
                    k.op("pool", lambda: nc.gpsimd.tensor_scalar(out=a_[:], in0=a_[:], scalar1=1.0, scalar2=None, op0=ALU.add),
                         reads=[ab_], writes=[ab_])
                    k.op("dve", lambda: nc.vector.reciprocal(out=a_[:], in_=a_[:]), reads=[ab_], writes=[ab_])
                    k.op("dve", lambda: nc.vector.tensor_scalar(out=a_[:], in0=a_[:], scalar1=oml[:, h:h + 1], scalar2=lb[:, h:h + 1],
                                                                op0=ALU.mult, op1=ALU.add),
                         reads=[ab_, omlb, lbb], writes=[ab_])
                    k.op("act", lambda: nc.scalar.activation(out=b_[:], in_=a_[:], func=AF.Ln), reads=[ab_], writes=[bb_])
                    k.op("pool", lambda: nc.gpsimd.tensor_scalar(out=k_[:], in0=a_[:], scalar1=-1.0, scalar2=1.0,
                                                                 op0=ALU.mult, op1=ALU.add),
                         reads=[ab_], writes=[kb_])
                    k.op("dve", lambda: nc.vector.tensor_tensor_scan(out=c_[:], data0=rmask[:], data1=b_[:], initial=0.0,
                                                                     op0=ALU.mult, op1=ALU.add),
                         reads=[rmaskb, bb_], writes=[cb_])
                    cv = c_[:].rearrange("p (n j) -> p n j", j=32)
                    if di == 1:
                        k.op("pool", lambda: nc.gpsimd.tensor_tensor(out=b_[:], in0=b_[:], in1=c_[:], op=ALU.subtract),
                             reads=[bb_, cb_], writes=[bb_])
                        k.op("dve", lambda: nc.vector.tensor_tensor(out=a_[:].rearrange("p (n j) -> p n j", j=32),
                                                                    in0=b_[:].rearrange("p (n j) -> p n j", j=32),
                                                                    in1=cv[:, :, 31:32].broadcast_to([128, 16, 32]), op=ALU.add),
                             reads=[bb_, cb_], writes=[ab_])
                        bsrc, bsrcb = a_, ab_
                        edge = 0
                    else:
                        bsrc, bsrcb = c_, cb_
                        edge = 31
                    k.op("act", lambda: nc.scalar.activation(out=e_[:], in_=bsrc[:], func=AF.Exp), reads=[bsrcb], writes=[eb_])
                    k.op("act", lambda: nc.scalar.activation(out=f_[:], in_=bsrc[:], func=AF.Exp, scale=-1.0), reads=[bsrcb], writes=[fb_])
                    k.op("dve", lambda: nc.vector.tensor_tensor(out=oq[:], in0=c.hg_qall[:, s * 512:(s + 1) * 512], in1=e_[:], op=ALU.mult),
                         reads=[c.hg_qallb, eb_], writes=[oqb])
                    k.op("pool", lambda: nc.gpsimd.tensor_tensor(out=f_[:], in0=k_[:], in1=f_[:], op=ALU.mult),
                         reads=[kb_, fb_], writes=[fb_])
                    k.op("pool", lambda: nc.gpsimd.tensor_copy(out=ok[:], in_=f_[:]), reads=[fb_], writes=[okb])
                    ev = e_[:].rearrange("p (n j) -> p n j", j=32)
                    k.op("dve", lambda: nc.vector.tensor_tensor(out=oi[:].rearrange("p (n j) -> p n j", j=32),
                                                                in0=f_[:].rearrange("p (n j) -> p n j", j=32),
                                                                in1=ev[:, :, edge:edge + 1].broadcast_to([128, 16, 32]), op=ALU.mult),
                         reads=[fb_, eb_], writes=[oib])
                    k.op("pool", lambda: nc.gpsimd.tensor_copy(out=od[:].rearrange("p (n o) -> p n o", o=1), in_=ev[:, :, edge:edge + 1]),
                         reads=[eb_], writes=[odb])
                    k.dma("pool", qi_d[di, h, :, s * 512:(s + 1) * 512], oq[:], reads=[oqb], owner=oqb)
                    k.dma("pool", kt_d[di, h, :, s * 512:(s + 1) * 512], ok[:], reads=[okb], owner=okb)
                    k.dma("pool", kin_d[di, h, :, s * 512:(s + 1) * 512], oi[:], reads=[oib], owner=oib)
                    k.dma("pool", dec_d[di, h, :, s * 16:(s + 1) * 16], od[:], reads=[odb], owner=odb)
                ws.prefetch(wi + 1)
                wi += 1
        nv = 0
        for vt in range(4):
            w, wb = ws.get(wi)
            wm = w[:, 0:2048].rearrange("p (c m) -> p c m", c=NC8)
            for tt in range(NT):
                ps, psb = next_bank(c)
                for cc in range(NC8):
                    k.op("pe", lambda: nc.tensor.matmul(ps[:, 0:256], lhsT=hT[:, cc, tt * 128:(tt + 1) * 128], rhs=wm[:, cc, :],
                                                        start=(cc == 0), stop=(cc == NC8 - 1)),
                         reads=[wb, hTb], writes=[psb], tick=(cc == NC8 - 1))
                t_, b_ = o_v[nv % 3]
                nv += 1
                k.op("act", lambda: nc.scalar.copy(out=t_[:], in_=ps[:, 0:256]), reads=[psb], writes=[b_])
                k.dma("pool", v_d[tt * 128:(tt + 1) * 128, vt * 256:(vt + 1) * 256], t_[:], reads=[b_], owner=b_)
            ws.prefetch(wi + 1)
            wi += 1
        k.barrier()
    with ExitStack() as st:
        gn, gnb = sb(c, st, "hg_gn", [128, 1], F32)
        k.dma("sp", gn[:], W["hg_g_norm"][ia].rearrange("(p o) -> p o", o=1), writes=[gnb], owner=gnb)
        ones128, ones128b = sb(c, st, "hg_ones", [128, 128], BF16)
        k.op("pool", lambda: nc.gpsimd.memset(ones128[:], 1.0 / 128.0), writes=[ones128b])
        idb, idbb = sb(c, st, "hg_idb", [128, 128], BF16)
        k.op("pool", lambda: nc.gpsimd.tensor_copy(out=idb[:], in_=c.ident[:]), reads=[c.identb], writes=[idbb])
        mk = []
        for di in range(2):
            m_, mb_ = sb(c, st, f"hg_mask{di}", [128, 128], F32)
            k.dma("sp", m_[:], c.hg_mask_d[di], writes=[mb_], owner=mb_)
            mk.append((m_, mb_))
        sets = []
        for i in range(2):
            d = {}
            for nm in ("qi0", "kt0", "kin0", "qi1", "kt1", "kin1", "sg"):
                d[nm] = sb(c, st, f"hg2_{nm}_{i}", [128, T], BF16)
            d["dec0"] = sb(c, st, f"hg2_dec0_{i}", [128, NCH], F32)
            d["dec1"] = sb(c, st, f"hg2_dec1_{i}", [128, NCH], F32)
            d["v"] = sb(c, st, f"hg2_v_{i}", [128, NT, 128], BF16)
            sets.append(d)
        osum, osumb = sb(c, st, "hg2_osum", [128, T], F32)
        S, Sb = sb(c, st, "hg2_S", [128, 128], F32)
        Sbf, Sbfb = sb(c, st, "hg2_Sbf", [128, 128], BF16)
        at = [sb(c, st, f"hg2_at{i}", [128, 128], BF16) for i in range(2)]
        kk_ = [sb(c, st, f"hg2_kk{i}", [128, 4, 128], BF16) for i in range(2)]
        cm, cmb = sb(c, st, "hg2_cm", [128, 4], F32)
        k.dma("sp", cm[:], c.hg_cmask_d[:, :], writes=[cmb], owner=cmb)
        sq = [sb(c, st, f"hg2_sq{i}", [128, 512], BF16) for i in range(2)]
        rs = [sb(c, st, f"hg2_rs{i}", [128, 512], F32) for i in range(2)]
        og = [sb(c, st, f"hg2_og{i}", [128, 512], F32) for i in range(2)]
        oo = [sb(c, st, f"hg2_oo{i}", [128, 512], BF16) for i in range(2)]

        def load_head(h):
            d = sets[h % 2]
            for di in range(2):
                k.dma("sp", d[f"qi{di}"][0][:], qi_d[di, h], writes=[d[f"qi{di}"][1]], owner=d[f"qi{di}"][1])
                k.dma("sp", d[f"kt{di}"][0][:], kt_d[di, h], writes=[d[f"kt{di}"][1]], owner=d[f"kt{di}"][1])
                k.dma("sp", d[f"kin{di}"][0][:], kin_d[di, h], writes=[d[f"kin{di}"][1]], owner=d[f"kin{di}"][1])
                k.dma("sp", d[f"dec{di}"][0][:], dec_d[di, h], writes=[d[f"dec{di}"][1]], owner=d[f"dec{di}"][1])
            k.dma("sp", d["sg"][0][:], sg_d[h * 128:(h + 1) * 128, :], writes=[d["sg"][1]], owner=d["sg"][1])
            k.dma("sp", d["v"][0][:], v_d[:, h * 128:(h + 1) * 128].rearrange("(t p) f -> p t f", p=128),
                  writes=[d["v"][1]], owner=d["v"][1])

        load_head(0)
        n = 0
        for h in range(NH):
            if h + 1 < NH:
                load_head(h + 1)
            d = sets[h % 2]
            v_, vb_ = d["v"]
            for di in range(2):
                qi, qib = d[f"qi{di}"]
                kt, ktb = d[f"kt{di}"]
                kin, kinb = d[f"kin{di}"]
                dec, decb = d[f"dec{di}"]
                m_, mb_ = mk[di]
                k.op("dve", lambda: nc.vector.memset(S[:], 0.0), writes=[Sb])
                k.op("pool", lambda: nc.gpsimd.memset(Sbf[:], 0.0), writes=[Sbfb])
                tiles = range(NT) if di == 0 else range(NT - 1, -1, -1)
                for tt in tiles:
                    c0 = tt * 128
                    a_, ab_ = at[n % 2]
                    kt_, ktb_ = kk_[n % 2]
                    n += 1
                    pT, pTb = next_bank(c)
                    k.op("pe", lambda: nc.tensor.matmul(pT[:, 0:128], lhsT=kin[:, c0:c0 + 128], rhs=idb[:], start=True, stop=True),
                         reads=[kinb, idbb], writes=[pTb])
                    k.op("dve", lambda: nc.vector.tensor_tensor(
                        out=kt_[:], in0=pT[:, 0:128].rearrange("p (o d) -> p o d", o=1).broadcast_to([128, 4, 128]),
                        in1=cm[:].rearrange("p (c o) -> p c o", o=1).broadcast_to([128, 4, 128]), op=ALU.mult),
                         reads=[pTb, cmb], writes=[ktb_])
                    pA, pAb = next_bank(c)
                    k.op("pe", lambda: nc.tensor.matmul(pA[:, 0:128], lhsT=kt[:, c0:c0 + 128], rhs=qi[:, c0:c0 + 128], start=True, stop=True),
                         reads=[ktb, qib], writes=[pAb])
                    k.op("dve", lambda: nc.vector.tensor_tensor(out=a_[:], in0=pA[:, 0:128], in1=m_[:], op=ALU.mult),
                         reads=[pAb, mb_], writes=[ab_])
                    pK, pKb = next_bank(c)
                    for cj in range(4):
                        k.op("pe", lambda: nc.tensor.matmul(pK[:, cj * 128:(cj + 1) * 128], lhsT=kt_[:, cj, :],
                                                            rhs=v_[:, tt, :], start=True, stop=True),
                             reads=[ktb_, vb_], writes=[pKb], tick=(cj == 3))
                    po, pob = next_bank(c)
                    k.op("pe", lambda: nc.tensor.matmul(po[:, 0:128], lhsT=v_[:, tt, :], rhs=a_[:], start=True, stop=False),
                         reads=[vb_, ab_], writes=[pob], tick=False)
                    chunks = range(4) if di == 0 else range(3, -1, -1)
                    for ci, cj in enumerate(chunks):
                        k.op("pe", lambda: nc.tensor.matmul(po[:, cj * 32:(cj + 1) * 32], lhsT=Sbf[:], rhs=qi[:, c0 + cj * 32:c0 + (cj + 1) * 32],
                                                            start=False, stop=(ci == 3)),
                             reads=[Sbfb, qib], writes=[pob], tick=True)
                        g = tt * 4 + cj
                        k.op("dve", lambda: nc.vector.scalar_tensor_tensor(out=S[:], in0=S[:], scalar=dec[:, g:g + 1],
                                                                           in1=pK[:, cj * 128:(cj + 1) * 128], op0=ALU.mult, op1=ALU.add),
                             reads=[Sb, decb, pKb], writes=[Sb])
                        k.op("act", lambda: nc.scalar.copy(out=Sbf[:], in_=S[:]), reads=[Sb], writes=[Sbfb])
                    if di == 0:
                        k.op("act", lambda: nc.scalar.copy(out=osum[:, c0:c0 + 128], in_=po[:, 0:128]), reads=[pob], writes=[osumb])
                    else:
                        k.op("dve", lambda: nc.vector.tensor_tensor(out=osum[:, c0:c0 + 128], in0=po[:, 0:128], in1=osum[:, c0:c0 + 128],
                                                                    op=ALU.add),
                             reads=[pob, osumb], writes=[osumb])
            sg, sgb = d["sg"]
            for s in range(T // 512):
                s_, sb_ = sq[s % 2]
                r_, rb_ = rs[s % 2]
                g_, gb_ = og[s % 2]
                o_, ob_ = oo[s % 2]
                sl = slice(s * 512, (s + 1) * 512)
                k.op("pool", lambda: nc.gpsimd.tensor_tensor(out=s_[:], in0=osum[:, sl], in1=osum[:, sl], op=ALU.mult),
                     reads=[osumb], writes=[sb_])
                ps, psb = next_bank(c)
                k.op("pe", lambda: nc.tensor.matmul(ps[:], lhsT=ones128[:], rhs=s_[:], start=True, stop=True),
                     reads=[sb_, ones128b], writes=[psb])
                k.op("dve", lambda: nc.vector.tensor_scalar(out=r_[:], in0=ps[:], scalar1=EPS, scalar2=None, op0=ALU.add),
                     reads=[psb], writes=[rb_])
                k.op("act", lambda: nc.scalar.activation(out=r_[:], in_=r_[:], func=AF.Sqrt), reads=[rb_], writes=[rb_])
                k.op("dve", lambda: nc.vector.reciprocal(out=r_[:], in_=r_[:]), reads=[rb_], writes=[rb_])
                k.op("dve", lambda: nc.vector.scalar_tensor_tensor(out=g_[:], in0=osum[:, sl], scalar=gn[:, 0:1], in1=r_[:],
                                                                   op0=ALU.mult, op1=ALU.mult),
                     reads=[osumb, gnb, rb_], writes=[gb_])
                k.op("pool", lambda: nc.gpsimd.tensor_tensor(out=o_[:], in0=g_[:], in1=sg[:, sl], op=ALU.mult),
                     reads=[gb_, sgb], writes=[ob_])
                k.dma("pool", oT_d[h * 128:(h + 1) * 128, sl], o_[:], reads=[ob_], owner=ob_)
        k.barrier()
    emit_outproj(c, T, oT_d, W["hg_w_out"][ia])
```
